# Optimizing a Trainium2 kernel written in Bass

```python
import math
import jax
import jax.numpy as jnp
from jax import lax
import numpy as np


D_MODEL = 1024
BATCH = 4
SEQ = 8192
DEPTH = 4

CTX_LEN = 256
GRID_W = 64
RET_HEADS = 4
RET_DK = 128
RET_DV = 128
RET_W = RET_HEADS * RET_DV
RET_CHUNK = 128
NA_HEADS = 8
NA_DH = 64
NA_W = NA_HEADS * NA_DH
NA_WIN_ROWS = 8
NA_WIN_COLS = 16
NA_ROW_BLOCK = 8
GQA_HEADS = 8
GQA_KV_HEADS = 2
GQA_DH = 64
GQA_W = GQA_HEADS * GQA_DH
GQA_KV_W = GQA_KV_HEADS * GQA_DH
Q_BLOCK = 128
ROPE_THETA = 10000.0
EPS = 1e-6
N_BRANCH = 3
IN_SPLITS = (RET_W, RET_W, RET_W, RET_W, NA_W, NA_W, NA_W, NA_W, GQA_W, GQA_KV_W, GQA_KV_W, GQA_W, N_BRANCH * D_MODEL)
IN_COLS = 4 * RET_W + 4 * NA_W + 2 * GQA_W + 2 * GQA_KV_W + N_BRANCH * D_MODEL

kernel_name = 'hybrid_retention_natten_gqa_prefix_trunk'


def rms_norm(x, w):
    xf = x.astype(jnp.float32)
    y = xf * lax.rsqrt(jnp.mean(xf * xf, axis=-1, keepdims=True) + EPS)
    return (y * w.astype(jnp.float32)).astype(x.dtype)


def heads(a, h):
    return a.reshape(a.shape[:-1] + (h, a.shape[-1] // h))


def split_cols(p):
    bounds = np.cumsum(IN_SPLITS)[:-1].tolist()
    return jnp.split(p, bounds, axis=-1)


def _flip(a, rev):
    return a[:, ::-1] if rev else a


def axial_rope(x):
    n, dh = x.shape[1], x.shape[-1]
    half = dh // 2
    quarter = half // 2
    t = jnp.arange(n)
    pos_r = (t // GRID_W).astype(jnp.float32)
    pos_c = (t % GRID_W).astype(jnp.float32)
    freqs = ROPE_THETA ** (-jnp.arange(quarter, dtype=jnp.float32) / quarter)

    def rot(xa, pos):
        ang = pos[:, None] * freqs[None, :]
        cos = jnp.cos(ang)[None, :, None, :]
        sin = jnp.sin(ang)[None, :, None, :]
        x1, x2 = xa[..., :quarter], xa[..., quarter:]
        return jnp.concatenate([x1 * cos - x2 * sin, x2 * cos + x1 * sin], axis=-1)

    xf = x.astype(jnp.float32)
    return jnp.concatenate([rot(xf[..., :half], pos_r), rot(xf[..., half:], pos_c)], axis=-1).astype(x.dtype)


def attend(q, k, v):
    s = jnp.einsum('btkgd,bskd->bkgts', q, k).astype(jnp.float32) * (q.shape[-1] ** -0.5)
    p = jax.nn.softmax(s, axis=-1)
    return jnp.einsum('bkgts,bskd->btkgd', p, v)


def ret_chunkwise(q, k, v, log_g, s0):
    b, t, h, dk = q.shape
    dv = v.shape[-1]
    c = RET_CHUNK
    n = t // c
    qc = q.reshape(b, n, c, h, dk)
    kc = k.reshape(b, n, c, h, dk)
    vc = v.reshape(b, n, c, h, dv)
    pos = jnp.arange(c, dtype=jnp.float32)
    diff = pos[:, None] - pos[None, :]
    decay = jnp.where(diff >= 0, jnp.exp(jnp.maximum(diff, 0.0)[None] * log_g[:, None, None]), 0.0)
    scores = jnp.einsum('bnihd,bnjhd->bnhij', qc, kc) * decay[None, None]
    intra = jnp.einsum('bnhij,bnjhe->bnihe', scores, vc)
    zeta = jnp.exp((c - 1 - pos)[None, :] * log_g[:, None])
    u = jnp.einsum('bnjhd,hj,bnjhe->nbhde', kc, zeta, vc)
    g_chunk = jnp.exp(c * log_g)[None, :, None, None]

    def step(s, u_n):
        return g_chunk * s + u_n, s

    s_fin, s_prev = lax.scan(step, s0, u)
    xi = jnp.exp((pos + 1.0)[None, :] * log_g[:, None])
    inter = jnp.einsum('bnihd,hi,nbhde->bnihe', qc, xi, s_prev)
    return (intra + inter).reshape(b, t, h, dv), s_fin


def ret_state(k, v, log_g):
    t = k.shape[1]
    pos = jnp.arange(t, dtype=jnp.float32)
    w = jnp.exp((t - 1 - pos)[None, :] * log_g[:, None])
    return jnp.einsum('bthd,ht,bthe->bhde', k, w, v)


def bidir_retention(q, k, v, qc, kc, vc, log_g, need_ctx):
    b, _, h, dk = q.shape
    dv = v.shape[-1]
    o_lat, o_ctx = [], []
    for d in range(2):
        rev = d == 1
        if need_ctx:
            s0 = jnp.zeros((b, h, dk, dv), jnp.float32)
            oc, s_ctx = ret_chunkwise(_flip(qc, rev), _flip(kc, rev), _flip(vc, rev), log_g[d], s0)
            o_ctx.append(_flip(oc, rev))
        else:
            s_ctx = ret_state(_flip(kc, rev), _flip(vc, rev), log_g[d])
        ol, _ = ret_chunkwise(_flip(q, rev), _flip(k, rev), _flip(v, rev), log_g[d], s_ctx)
        o_lat.append(_flip(ol, rev))
    return o_lat[0] + o_lat[1], (o_ctx[0] + o_ctx[1] if need_ctx else None)


def head_group_norm(o, w):
    mu = jnp.mean(o, axis=-1, keepdims=True)
    var = jnp.mean(jnp.square(o - mu), axis=-1, keepdims=True)
    y = (o - mu) * lax.rsqrt(var + EPS)
    return y.reshape(o.shape[0], o.shape[1], -1) * w.astype(jnp.float32)


def neighbourhood_attention(q, k, v, kc, vc, rpb, rows):
    b, n, h, dh = q.shape
    wr = min(NA_WIN_ROWS, rows)
    qcb = NA_WIN_COLS
    ncb = GRID_W // qcb
    band = 2 * qcb
    qcol = np.arange(ncb)[:, None] * qcb + np.arange(qcb)[None, :]
    cstart = np.clip(qcol - NA_WIN_COLS // 2, 0, GRID_W - NA_WIN_COLS)
    bstart = np.clip(np.arange(ncb) * qcb - NA_WIN_COLS // 2, 0, GRID_W - band)
    kcol = bstart[:, None] + np.arange(band)[None, :]
    col_ok = jnp.asarray((kcol[:, None, :] >= cstart[:, :, None]) & (kcol[:, None, :] < cstart[:, :, None] + NA_WIN_COLS))
    dc_idx = np.clip(kcol[:, None, :] - qcol[:, :, None] + NA_WIN_COLS - 1, 0, 2 * NA_WIN_COLS - 2)
    rpb_c = rpb.astype(jnp.float32)[:, :, dc_idx]

    kg = k.reshape(b, rows, GRID_W, h, dh)[:, :, kcol]
    vg = v.reshape(b, rows, GRID_W, h, dh)[:, :, kcol]
    rb = math.gcd(rows, NA_ROW_BLOCK)
    nrb = rows // rb
    qblocks = q.reshape(b, nrb, rb, ncb, qcb, h, dh).transpose(1, 0, 2, 3, 4, 5, 6)
    scale = dh ** -0.5
    n_loc = wr * band

    def block(args):
        i, qb = args
        r = i * rb + jnp.arange(rb)
        rs = jnp.clip(r - wr // 2, 0, rows - wr)
        ridx = rs[:, None] + jnp.arange(wr)[None, :]
        kb = jnp.take(kg, ridx, axis=1)
        vb = jnp.take(vg, ridx, axis=1)
        dr_idx = ridx - r[:, None] + NA_WIN_ROWS - 1
        bias = rpb_c[:, dr_idx].transpose(0, 1, 3, 4, 2, 5)
        s = jnp.einsum('brjqhd,brwjkhd->bhrjqwk', qb, kb).astype(jnp.float32) * scale + bias[None]
        s = jnp.where(col_ok[:, :, None, :], s, -jnp.inf)
        s_ctx = jnp.einsum('brjqhd,blhd->bhrjql', qb, kc).astype(jnp.float32) * scale
        s_all = jnp.concatenate([s.reshape(s.shape[:5] + (n_loc,)), s_ctx], axis=-1)
        p = jax.nn.softmax(s_all, axis=-1)
        p_loc = p[..., :n_loc].reshape(s.shape)
        p_ctx = p[..., n_loc:]
        return jnp.einsum('bhrjqwk,brwjkhd->brjqhd', p_loc, vb) + jnp.einsum('bhrjql,blhd->brjqhd', p_ctx, vc)

    o = lax.map(block, (jnp.arange(nrb), qblocks))
    return o.transpose(1, 0, 2, 3, 4, 5, 6).reshape(b, n, h * dh)


def gqa_blocks(q, k_all, v_all):
    b, n, hq, dh = q.shape
    hk = k_all.shape[2]
    nb = n // Q_BLOCK
    qb = q.reshape(b, nb, Q_BLOCK, hk, hq // hk, dh).transpose(1, 0, 2, 3, 4, 5)
    o = lax.map(lambda qi: attend(qi, k_all, v_all), qb)
    return o.transpose(1, 0, 2, 3, 4, 5).reshape(b, n, hq * dh)


def merge_branches(ret_o, na_o, gqa_o, gate_logits, w_ret_o, w_na_o, w_gqa_o, w_out):
    dt = w_out.dtype
    g_ret, g_na, g_gqa = jnp.split(jax.nn.sigmoid(gate_logits.astype(jnp.float32)), N_BRANCH, axis=-1)
    m = (g_ret * (ret_o.astype(dt) @ w_ret_o) + g_na * (na_o.astype(dt) @ w_na_o)
         + g_gqa * (gqa_o.astype(dt) @ w_gqa_o))
    return m.astype(dt) @ w_out


def hybrid_layer(x, y, c_silu, cc_silu, ada_w, ada_b, norm_w, w_in, ret_log_decay, ret_gn_w, na_rpb,
                 q_norm_w, k_norm_w, w_ret_o, w_na_o, w_gqa_o, w_out, need_ctx):
    b, n, _ = x.shape
    l = y.shape[1]
    rows = n // GRID_W
    g_grp = GQA_HEADS // GQA_KV_HEADS
    shift_x, scale_x, gate_x = jnp.split((c_silu @ ada_w + ada_b)[:, None, :], 3, axis=-1)
    shift_y, scale_y, gate_y = jnp.split(cc_silu @ ada_w + ada_b, 3, axis=-1)
    hx = rms_norm(x, norm_w) * (1.0 + scale_x) + shift_x
    hy = rms_norm(y, norm_w) * (1.0 + scale_y) + shift_y
    rq, rk, rv, rg, nq, nk, nv, ng, gq, gk, gv, gg, mg = split_cols(hx @ w_in)
    crq, crk, crv, crg, cnq, cnk, cnv, cng, cgq, cgk, cgv, cgg, cmg = split_cols(hy @ w_in)

    log_g = -jnp.exp(ret_log_decay.astype(jnp.float32))
    ksc = RET_DK ** -0.5
    o_lat, o_ctx = bidir_retention(heads(rq, RET_HEADS), heads(rk, RET_HEADS) * ksc, heads(rv, RET_HEADS),
                                   heads(crq, RET_HEADS), heads(crk, RET_HEADS) * ksc, heads(crv, RET_HEADS),
                                   log_g, need_ctx)
    ret_x = head_group_norm(o_lat, ret_gn_w) * jax.nn.silu(rg)

    kc_na = heads(cnk, NA_HEADS)
    vc_na = heads(cnv, NA_HEADS)
    na_x = neighbourhood_attention(heads(nq, NA_HEADS), heads(nk, NA_HEADS), heads(nv, NA_HEADS),
                                   kc_na, vc_na, na_rpb, rows) * jax.nn.silu(ng)

    kc_g = rms_norm(heads(cgk, GQA_KV_HEADS), k_norm_w)
    vc_g = heads(cgv, GQA_KV_HEADS)
    q_g = axial_rope(rms_norm(heads(gq, GQA_HEADS), q_norm_w))
    k_g = axial_rope(rms_norm(heads(gk, GQA_KV_HEADS), k_norm_w))
    k_all = jnp.concatenate([k_g, kc_g], axis=1)
    v_all = jnp.concatenate([heads(gv, GQA_KV_HEADS), vc_g], axis=1)
    gqa_x = gqa_blocks(q_g, k_all, v_all) * jax.nn.silu(gg)

    x = x + gate_x * merge_branches(ret_x, na_x, gqa_x, mg, w_ret_o, w_na_o, w_gqa_o, w_out)

    if need_ctx:
        ret_y = head_group_norm(o_ctx, ret_gn_w) * jax.nn.silu(crg)
        na_y = attend(heads(cnq, NA_HEADS)[:, :, :, None, :], kc_na, vc_na).reshape(b, l, NA_W) * jax.nn.silu(cng)
        qc_g = rms_norm(heads(cgq, GQA_HEADS), q_norm_w).reshape(b, l, GQA_KV_HEADS, g_grp, GQA_DH)
        gqa_y = attend(qc_g, kc_g, vc_g).reshape(b, l, GQA_W) * jax.nn.silu(cgg)
        y = y + gate_y * merge_branches(ret_y, na_y, gqa_y, cmg, w_ret_o, w_na_o, w_gqa_o, w_out)
    return x, y


def setup_inputs(seed: int = 0) -> dict:
    key = jax.random.key(seed)
    ks = jax.random.split(key, 18)
    f32 = jnp.float32
    d = D_MODEL

    def nrm(k, shape, scale):
        return jax.random.normal(k, shape, f32) * scale

    decay_base = jnp.asarray(np.log(-np.log(1.0 - 2.0 ** (-5.0 - np.arange(RET_HEADS)))), f32)
    return {
        'x': nrm(ks[0], (BATCH, SEQ, d), 1.0),
        'c': nrm(ks[1], (BATCH, d), 1.0),
        'ctx': nrm(ks[2], (BATCH, CTX_LEN, d), 1.0),
        'c_ctx': nrm(ks[3], (d,), 1.0),
        'ada_w': nrm(ks[4], (DEPTH, d, 3 * d), d ** -0.5),
        'ada_b': nrm(ks[5], (DEPTH, 3 * d), 0.02),
        'norm_w': 1.0 + nrm(ks[6], (DEPTH, d), 0.02),
        'w_in': nrm(ks[7], (DEPTH, d, IN_COLS), d ** -0.5),
        'ret_log_decay': decay_base + nrm(ks[8], (DEPTH, 2, RET_HEADS), 0.1),
        'ret_gn_w': 1.0 + nrm(ks[9], (DEPTH, RET_W), 0.02),
        'na_rpb': nrm(ks[10], (DEPTH, NA_HEADS, 2 * NA_WIN_ROWS - 1, 2 * NA_WIN_COLS - 1), 0.05),
        'q_norm_w': 1.0 + nrm(ks[11], (DEPTH, GQA_DH), 0.02),
        'k_norm_w': 1.0 + nrm(ks[12], (DEPTH, GQA_DH), 0.02),
        'w_ret_o': nrm(ks[13], (DEPTH, RET_W, d), RET_W ** -0.5),
        'w_na_o': nrm(ks[14], (DEPTH, NA_W, d), NA_W ** -0.5),
        'w_gqa_o': nrm(ks[15], (DEPTH, GQA_W, d), GQA_W ** -0.5),
        'w_out': nrm(ks[16], (DEPTH, d, d), d ** -0.5),
        'final_norm_w': 1.0 + nrm(ks[17], (d,), 0.02),
    }


def reference(x, c, ctx, c_ctx, ada_w, ada_b, norm_w, w_in, ret_log_decay, ret_gn_w, na_rpb,
              q_norm_w, k_norm_w, w_ret_o, w_na_o, w_gqa_o, w_out, final_norm_w):
    c_silu = jax.nn.silu(c)
    cc_silu = jax.nn.silu(c_ctx)
    y = ctx
    for layer in range(DEPTH):
        x, y = hybrid_layer(x, y, c_silu, cc_silu, ada_w[layer], ada_b[layer], norm_w[layer], w_in[layer],
                            ret_log_decay[layer], ret_gn_w[layer], na_rpb[layer], q_norm_w[layer], k_norm_w[layer],
                            w_ret_o[layer], w_na_o[layer], w_gqa_o[layer], w_out[layer],
                            need_ctx=layer < DEPTH - 1)
    return rms_norm(x, final_norm_w)
```

```python
import contextlib
import numpy as np
import concourse.bass as bass
import concourse.mybir as mybir
from concourse.bass_utils import run_bass_kernel_spmd

F32 = mybir.dt.float32
BF16 = mybir.dt.bfloat16
ALU = mybir.AluOpType
AF = mybir.ActivationFunctionType
AX = mybir.AxisListType

D = 1024
KC = 8
L = 256
GW = 64
EPS = 1e-6
IN_COLS = 8448
C_RQ, C_RK, C_RV, C_RG = 0, 512, 1024, 1536
C_NQ, C_NK, C_NV, C_NG = 2048, 2560, 3072, 3584
C_GQ, C_GK, C_GV, C_GG, C_MG = 4096, 4608, 4736, 4864, 5376
NEG = -30000.0
DEBUG = False
N_DMA_SEMS = 48


class Buf:
    __slots__ = ("last_w", "readers")

    def __init__(self):
        self.last_w = None
        self.readers = []


class Op:
    __slots__ = ("eng", "sem", "val", "is_dma")

    def __init__(self, eng, sem, val, is_dma):
        self.eng, self.sem, self.val, self.is_dma = eng, sem, val, is_dma


class Prog:
    ENGS = ("pe", "act", "dve", "pool", "sp")

    def __init__(self, nc, st):
        self.nc = nc
        self.recs = {e: [] for e in self.ENGS}
        self.cnt = {e: 0 for e in self.ENGS}
        self.waited = {e: {} for e in self.ENGS}
        self.dma_cnt = [0] * N_DMA_SEMS
        self.dma_last = [None] * N_DMA_SEMS
        self.dma_rr = 0
        self.sems = {}
        for e in self.ENGS:
            if e != "sp":
                self.sems[("e", e)] = st.enter_context(nc.semaphore("s_" + e))
        for si in range(N_DMA_SEMS):
            self.sems[("d", si)] = st.enter_context(nc.semaphore("d_%d" % si))

    def add(self, eng, fn, reads=(), writes=(), dma=False):
        waits = {}

        def need(d):
            if (not d.is_dma) and (not dma) and d.eng == eng and eng == "pe":
                return
            if waits.get(d.sem, 0) < d.val:
                waits[d.sem] = d.val

        for b in reads:
            if b.last_w is not None:
                need(b.last_w)
        for b in writes:
            if b.last_w is not None:
                need(b.last_w)
            for r in b.readers:
                need(r)
        if dma:
            si = self.dma_rr
            self.dma_rr = (self.dma_rr + 1) % N_DMA_SEMS
            prev = self.dma_last[si]
            if prev is not None:
                need(prev)
            self.dma_cnt[si] += 16
            op = Op(eng, ("d", si), self.dma_cnt[si], True)
            self.dma_last[si] = op
        else:
            self.cnt[eng] += 1
            op = Op(eng, ("e", eng), self.cnt[eng], False)
        w = self.waited[eng]
        wl = []
        for k, v in waits.items():
            if w.get(k, 0) < v:
                w[k] = v
                wl.append((k, v))
        self.recs[eng].append((wl, fn, op))
        for b in reads:
            b.readers.append(op)
        for b in writes:
            b.last_w = op
            b.readers = []
        return op

    def pe(self, fn, reads=(), writes=()):
        return self.add("pe", fn, reads, writes)

    def act(self, fn, reads=(), writes=()):
        return self.add("act", fn, reads, writes)

    def dve(self, fn, reads=(), writes=()):
        return self.add("dve", fn, reads, writes)

    def pool(self, fn, reads=(), writes=()):
        return self.add("pool", fn, reads, writes)

    def dma(self, out, in_, reads=(), writes=(), slow=False):
        if slow:
            return self.add("sp", lambda e: e.dma_start(out=out, in_=in_, allow_slow_non_contiguous=True), reads, writes, dma=True)
        return self.add("sp", lambda e: e.dma_start(out=out, in_=in_), reads, writes, dma=True)

    def barrier(self):
        allw = {}
        for e in self.ENGS:
            if e != "sp" and self.cnt[e] > 0:
                allw[("e", e)] = self.cnt[e]
        for si in range(N_DMA_SEMS):
            if self.dma_cnt[si] > 0:
                allw[("d", si)] = self.dma_cnt[si]
        for e in self.ENGS:
            w = self.waited[e]
            wl = []
            for k, v in allw.items():
                if k == ("e", e):
                    continue
                if w.get(k, 0) < v:
                    w[k] = v
                    wl.append((k, v))
            if wl:
                self.recs[e].append((wl, None, None))

    def flush(self):
        self.barrier()
        sems = self.sems
        recs = self.recs
        with self.nc.Block() as block:
            def run(engname):
                def body(e):
                    for wl, fn, op in recs[engname]:
                        if fn is None:
                            for k, v in wl:
                                e.wait_ge(sems[k], v)
                            continue
                        for k, v in wl[1:]:
                            e.wait_ge(sems[k], v)
                        ins = fn(e)
                        if wl:
                            ins._wait_ge(sems[wl[0][0]], wl[0][1])
                        ins.then_inc(sems[op.sem], 16 if op.is_dma else 1)
                return body

            block.tensor(run("pe"))
            block.scalar(run("act"))
            block.vector(run("dve"))
            block.gpsimd(run("pool"))
            block.sync(run("sp"))
        self.recs = {e: [] for e in self.ENGS}


def _na_valid(rows, r, kr, c, kc):
    rs = np.clip(r - 4, 0, rows - 8)
    cs = np.clip(c - 8, 0, GW - 16)
    return (kr >= rs) & (kr < rs + 8) & (kc >= cs) & (kc < cs + 16)


def host_consts(NT):
    rows = NT // GW
    TT = NT + L
    cst = {}
    t = np.arange(NT)
    pos_r = (t // GW).astype(np.float32)
    pos_c = (t % GW).astype(np.float32)
    freqs = (10000.0 ** (-np.arange(16, dtype=np.float32) / 16)).astype(np.float32)
    ar = pos_r[:, None] * freqs[None, :]
    ac = pos_c[:, None] * freqs[None, :]
    C = np.ones((TT, 64), np.float32)
    S = np.zeros((TT, 64), np.float32)
    C[:NT] = np.concatenate([np.cos(ar), np.cos(ar), np.cos(ac), np.cos(ac)], 1)
    S[:NT] = np.concatenate([-np.sin(ar), np.sin(ar), -np.sin(ac), np.sin(ac)], 1)
    cst["ropeC"] = C
    cst["ropeS"] = S
    j = np.arange(128, dtype=np.float32)
    pt = np.zeros((128, 8), np.float32)
    pt[:, 0] = 127 - j
    pt[:, 1] = j
    pt[:, 2] = j + 1
    pt[:, 3] = 128 - j
    pt[:, 4] = 128.0
    cst["rpos"] = pt
    dj = j[:, None]
    di = j[None, :]
    cst["dmf"] = np.maximum(di - dj, 0).astype(np.float32)
    cst["dmb"] = np.maximum(dj - di, 0).astype(np.float32)
    cst["mkf"] = (di >= dj).astype(np.float32)
    cst["mkb"] = (dj >= di).astype(np.float32)
    krl = (np.arange(128) // 64)[:, None, None]
    kc = (np.arange(128) % 64)[:, None, None]
    m = np.arange(22)[None, :, None]
    c = np.arange(64)[None, None, :]
    dr = krl + 10 - m
    cs = np.clip(c - 8, 0, GW - 16)
    vi = (dr >= -4) & (dr <= 3) & (kc >= cs) & (kc < cs + 16)
    cst["maskI"] = np.where(vi, 0.0, NEG).astype(np.float32).reshape(128, 22 * 64)
    mE = np.zeros((128, 12, 8, 64), np.float32)
    rl = np.arange(8)[None, :, None]
    krl2 = (np.arange(128) // 64)[:, None, None]
    kc2 = (np.arange(128) % 64)[:, None, None]
    for e in range(12):
        if e < 6:
            i, u = 0, e + 2
        else:
            i, u = rows // 8 - 1, e - 6
        r = 8 * i + rl
        kr = 8 * i - 4 + 2 * u + krl2
        v = _na_valid(rows, r, kr, c, kc2)
        mE[:, e] = np.where(v, 0.0, NEG)
    cst["maskE"] = mE.reshape(128, 12 * 512)
    return cst


def gather_rpb(na_rpb):
    krl = (np.arange(128) // 64)[:, None, None]
    kc = (np.arange(128) % 64)[:, None, None]
    m = np.arange(22)[None, :, None]
    c = np.arange(64)[None, None, :]
    i0 = np.clip(krl + 10 - m + 7, 0, 14) + 0 * c
    i1 = np.clip(kc - c + 15, 0, 30) + 0 * m
    g = na_rpb[:, :, i0, i1]
    return np.ascontiguousarray(g.transpose(0, 2, 1, 3, 4)).reshape(na_rpb.shape[0], 128, 8 * 22 * 64)


def build(NT, DEPTH):
    TT = NT + L
    NTL = NT // 128
    NTI = TT // 128
    ROWS = NT // GW
    NRB = ROWS // 8
    nc = bass.Bass("TRN2", target_bir_lowering=False)

    def din(name, shape, dt=F32):
        return nc.dram_tensor(name, list(shape), dt, kind="ExternalInput").ap()

    def dscr(name, shape, dt):
        return nc.dram_tensor(name, list(shape), dt, kind="Internal").ap()

    x_in = din("x", [NT, D])
    c_in = din("c", [D])
    ctx_in = din("ctx", [L, D])
    cctx_in = din("c_ctx", [D])
    ada_w = din("ada_w", [DEPTH, D, 3 * D])
    ada_b = din("ada_b", [DEPTH, 3 * D])
    norm_w = din("norm_w", [DEPTH, D])
    w_in = din("w_in", [DEPTH, D, IN_COLS])
    rld = din("ret_log_decay", [DEPTH, 8])
    gn_w = din("ret_gn_w", [DEPTH, 512])
    rpbg = din("rpbg", [DEPTH, 128, 8 * 22 * 64])
    qn_w = din("q_norm_w", [DEPTH, 64])
    kn_w = din("k_norm_w", [DEPTH, 64])
    w_ro = din("w_ret_o", [DEPTH, 512, D])
    w_no = din("w_na_o", [DEPTH, 512, D])
    w_go = din("w_gqa_o", [DEPTH, 512, D])
    w_out = din("w_out", [DEPTH, D, D])
    fnw = din("final_norm_w", [D])
    ropeC = din("ropeC", [TT, 64])
    ropeS = din("ropeS", [TT, 64])
    rpos = din("rpos", [128, 8])
    dmf_in = din("dmf", [128, 128])
    dmb_in = din("dmb", [128, 128])
    mkf_in = din("mkf", [128, 128])
    mkb_in = din("mkb", [128, 128])
    maskI_in = din("maskI", [128, 22 * 64])
    maskE_in = din("maskE", [128, 12 * 512])
    out = nc.dram_tensor("out", [NT, D], F32, kind="ExternalOutput").ap()

    xs = dscr("xs", [TT, D], F32)
    hxT = dscr("hxT", [D, TT], BF16)
    retK = dscr("retK", [TT, 512], BF16)
    retV = dscr("retV", [TT, 512], BF16)
    naV = dscr("naV", [TT, 1024], BF16)
    gV = dscr("gV", [TT, 256], BF16)
    gKT = dscr("gKT", [128, TT], BF16)
    naKT = dscr("naKT", [512, TT], BF16)
    SbP = dscr("SbP", [NTI, 128, 512], BF16)
    retXT = dscr("retXT", [512, TT], BF16)
    naXT = dscr("naXT", [512, TT], BF16)
    gqaXT = dscr("gqaXT", [512, TT], BF16)
    DBG = {}
    if DEBUG:
        DBG["mT"] = dscr("dbg_mT", [TT // 256, 128, KC * 256], BF16)
    bxs = Buf()

    with contextlib.ExitStack() as gst:
        p = Prog(nc, gst)

        uid = [0]

        def mk(st):
            def sb(name, shape, dt):
                uid[0] += 1
                return st.enter_context(nc.sbuf_tensor("%s_%d" % (name, uid[0]), list(shape), dt)), Buf()

            def ps(name, shape, dt=F32):
                uid[0] += 1
                return st.enter_context(nc.psum_tensor("%s_%d" % (name, uid[0]), list(shape), dt)), Buf()
            return sb, ps

        gsb, _ = mk(gst)
        ident, b_ident = gsb("ident", [128, 128], BF16)
        onesb, b_ones = gsb("onesb", [128, 128], F32)
        modt, b_modt = gsb("modt", [128, 6, D], F32)
        csb, b_csb = gsb("csb", [128, 2, KC, 128], BF16)
        p.pool(lambda e: e.memset(ident[:], 0.0), writes=[b_ident])
        p.pool(lambda e: e.affine_select(ident[:], ident[:], pattern=[[-1, 128]], compare_op=ALU.not_equal,
                                         fill=1.0, base=0, channel_multiplier=1), reads=[b_ident], writes=[b_ident])
        p.pool(lambda e: e.memset(onesb[:], 1.0), writes=[b_ones])

        def load_w(st, dst, b_dst, col0, src_fn, ncols, tag, nk=KC, perm=None):
            sb, _ = mk(st)
            PIECE = 1024
            stg = [sb("wst%s%d" % (tag, i), [128, PIECE], F32) for i in range(2)]
            n = 0
            for k in range(nk):
                for c0 in range(0, ncols, PIECE):
                    cw = min(PIECE, ncols - c0)
                    s, bs = stg[n % 2]
                    src = src_fn(k, c0, cw)
                    if perm is not None:
                        for o_ap, i_ap in perm(s[:, 0:cw], src):
                            p.dma(o_ap, i_ap, writes=[bs])
                    else:
                        p.dma(s[:, 0:cw], src, writes=[bs])
                    d_ap = dst[:, k, col0 + c0:col0 + c0 + cw]
                    if n % 2 == 0:
                        p.dve(lambda e, d_ap=d_ap, s=s, cw=cw: e.tensor_copy(d_ap, s[:, 0:cw]), reads=[bs], writes=[b_dst])
                    else:
                        p.pool(lambda e, d_ap=d_ap, s=s, cw=cw: e.tensor_copy(d_ap, s[:, 0:cw]), reads=[bs], writes=[b_dst])
                    n += 1

        def win_src(l, col):
            return lambda k, c0, cw: w_in[l, k * 128:(k + 1) * 128, col + c0:col + c0 + cw]

        def xtile_src(l, t):
            if l == 0:
                if t < NTL:
                    return x_in[t * 128:(t + 1) * 128, :]
                return ctx_in[(t - NTL) * 128:(t - NTL + 1) * 128, :]
            return xs[t * 128:(t + 1) * 128, :]

        def pass_setup():
            with contextlib.ExitStack() as st:
                sb, ps = mk(st)
                cs, bcs = sb("cs", [128, 2, KC], F32)
                p.dma(cs[:, 0, :], c_in.rearrange("(k q) -> q k", q=128), writes=[bcs], slow=True)
                p.dma(cs[:, 1, :], cctx_in.rearrange("(k q) -> q k", q=128), writes=[bcs], slow=True)
                p.act(lambda e: e.activation(cs[:], cs[:], AF.Silu), reads=[bcs], writes=[bcs])
                for a in range(2):
                    for k in range(KC):
                        p.dve(lambda e, a=a, k=k: e.tensor_scalar(csb[:, a, k, :], onesb[:], cs[:, a, k:k + 1], None, ALU.mult),
                              reads=[bcs, b_ones], writes=[b_csb])
                p.flush()

        def pass_adaln(l):
            with contextlib.ExitStack() as st:
                sb, ps = mk(st)
                wst = [sb("aw%d" % i, [128, KC, 512], F32) for i in range(2)]
                wbf = [sb("awb%d" % i, [128, KC, 512], BF16) for i in range(2)]
                bb, bbb = sb("adab", [128, 3 * D], F32)
                nw, bnw = sb("nw", [128, D], F32)
                pm = [ps("pm%d" % i, [128, 512]) for i in range(4)]
                p.dma(bb[:], ada_b[l].partition_broadcast(128), writes=[bbb])
                p.dma(nw[:], norm_w[l].partition_broadcast(128), writes=[bnw])
                for g in range(6):
                    s, bs = wst[g % 2]
                    wb, bwb = wbf[g % 2]
                    p.dma(s[:], ada_w[l, :, g * 512:(g + 1) * 512].rearrange("(k q) n -> q k n", q=128), writes=[bs])
                    p.dve(lambda e, wb=wb, s=s: e.tensor_copy(wb[:, 0:4, :], s[:, 0:4, :]), reads=[bs], writes=[bwb])
                    p.pool(lambda e, wb=wb, s=s: e.tensor_copy(wb[:, 4:8, :], s[:, 4:8, :]), reads=[bs], writes=[bwb])
                    which = g // 2
                    half = g % 2
                    for a in range(2):
                        pt, bpt = pm[(g * 2 + a) % 4]
                        for k in range(KC):
                            p.pe(lambda e, pt=pt, a=a, k=k, wb=wb: e.matmul(pt[:], lhsT=csb[:, a, k, :], rhs=wb[:, k, :],
                                                                          start=(k == 0), stop=(k == KC - 1)),
                                 reads=[b_csb, bwb], writes=[bpt])
                        slot = {1: 0, 0: 1, 2: 2}[which] + 3 * a
                        dst = modt[:, slot, half * 512:(half + 1) * 512]
                        bsl = bb[:, g * 512:(g + 1) * 512]
                        p.dve(lambda e, dst=dst, pt=pt, bsl=bsl: e.tensor_tensor(dst, pt[:], bsl, ALU.add),
                              reads=[bpt, bbb], writes=[b_modt])
                for a in range(2):
                    dst = modt[:, 3 * a, :]
                    p.dve(lambda e, dst=dst: e.scalar_tensor_tensor(dst, dst, 1.0, nw[:], ALU.add, ALU.mult),
                          reads=[b_modt, bnw], writes=[b_modt])
                p.flush()

        def pass_norm(l):
            with contextlib.ExitStack() as st:
                sb, ps = mk(st)
                xt = [sb("xt%d" % i, [128, D], F32) for i in range(2)]
                junk, bj = sb("junk", [128, D], F32)
                ssq = [sb("ssq%d" % i, [128, 1], F32) for i in range(2)]
                hb = [sb("hb%d" % i, [128, D], BF16) for i in range(2)]
                hf, bhf = sb("hf", [128, D], F32)
                ht = [sb("ht%d" % i, [128, KC, 128], BF16) for i in range(2)]
                pt = [ps("ptr%d" % i, [128, KC, 128], BF16) for i in range(2)]
                p.dma(xt[0][0][:], xtile_src(l, 0), reads=[bxs], writes=[xt[0][1]])
                for t in range(NTI):
                    x_, bx = xt[t % 2]
                    if t + 1 < NTI:
                        p.dma(xt[(t + 1) % 2][0][:], xtile_src(l, t + 1), reads=[bxs], writes=[xt[(t + 1) % 2][1]])
                    sq, bsq = ssq[t % 2]
                    h_, bh = hb[t % 2]
                    a = 0 if t < NTL else 1
                    p.act(lambda e, x_=x_, sq=sq: e.activation(junk[:], x_[:], AF.Square, accum_out=sq[:]),
                          reads=[bx], writes=[bj, bsq])
                    p.dve(lambda e, sq=sq: e.tensor_scalar(sq[:], sq[:], 1.0 / D, EPS, ALU.mult, ALU.add), reads=[bsq], writes=[bsq])
                    p.act(lambda e, sq=sq: e.activation(sq[:], sq[:], AF.Sqrt), reads=[bsq], writes=[bsq])
                    p.dve(lambda e, sq=sq: e.reciprocal(sq[:], sq[:]), reads=[bsq], writes=[bsq])
                    p.dve(lambda e, x_=x_, sq=sq, a=a: e.scalar_tensor_tensor(hf[:], x_[:], sq[:, 0:1], modt[:, 3 * a, :], ALU.mult, ALU.mult),
                          reads=[bx, bsq, b_modt], writes=[bhf])
                    p.pool(lambda e, h_=h_, a=a: e.tensor_tensor(h_[:], hf[:], modt[:, 3 * a + 1, :], ALU.add),
                           reads=[bhf, b_modt], writes=[bh])
                    pt_, bpt = pt[t % 2]
                    for k in range(KC):
                        p.pe(lambda e, pt_=pt_, h_=h_, k=k: e.transpose(pt_[:, k, :], h_[:, k * 128:(k + 1) * 128], ident[:]),
                             reads=[bh, b_ident], writes=[bpt])
                    ht_, bht = ht[t % 2]
                    p.act(lambda e, ht_=ht_, pt_=pt_: e.copy(ht_[:], pt_[:]), reads=[bpt], writes=[bht])
                    p.dma(hxT[:, t * 128:(t + 1) * 128].rearrange("(k q) t -> q k t", q=128), ht_[:], reads=[bht], writes=[bxs])
                p.flush()

        def rms_rope(sb_tmp, src_ps, bsrc, nh, wrow, bw, Ct, St, btab, t, dst, bdst, tag):
            (qf, bqf), (sqt, bsqt), (ss, bss), (t1, bt1), (t2, bt2) = sb_tmp
            W = nh * 64
            p.act(lambda e: e.copy(qf[:, 0:W], src_ps), reads=[bsrc], writes=[bqf])
            p.dve(lambda e: e.tensor_tensor(sqt[:, 0:W], qf[:, 0:W], qf[:, 0:W], ALU.mult), reads=[bqf], writes=[bsqt])
            p.dve(lambda e: e.tensor_reduce(ss[:, 0:nh], sqt[:, 0:W].rearrange("q (h d) -> q h d", d=64), AX.X, ALU.add),
                  reads=[bsqt], writes=[bss])
            p.dve(lambda e: e.tensor_scalar(ss[:, 0:nh], ss[:, 0:nh], 1.0 / 64, EPS, ALU.mult, ALU.add), reads=[bss], writes=[bss])
            p.act(lambda e: e.activation(ss[:, 0:nh], ss[:, 0:nh], AF.Sqrt), reads=[bss], writes=[bss])
            p.dve(lambda e: e.reciprocal(ss[:, 0:nh], ss[:, 0:nh]), reads=[bss], writes=[bss])
            q3 = qf[:, 0:W].rearrange("q (h d) -> q h d", d=64)
            p.dve(lambda e: e.tensor_tensor(q3, q3, ss[:, 0:nh].unsqueeze(2).to_broadcast([128, nh, 64]), ALU.mult),
                  reads=[bqf, bss], writes=[bqf])
            p.pool(lambda e: e.tensor_tensor(q3, q3, wrow[:, :].unsqueeze(1).to_broadcast([128, nh, 64]), ALU.mult),
                   reads=[bqf, bw], writes=[bqf])
            Cb = Ct[:, t, :].unsqueeze(1).to_broadcast([128, nh, 64])
            t13 = t1[:, 0:W].rearrange("q (h d) -> q h d", d=64)
            p.dve(lambda e: e.tensor_tensor(t13, q3, Cb, ALU.mult), reads=[bqf, btab], writes=[bt1])
            q5 = qf[:, 0:W].rearrange("q (h a b d) -> q h a b d", a=2, b=2, d=16)
            t25 = t2[:, 0:W].rearrange("q (h a b d) -> q h a b d", a=2, b=2, d=16)
            S5 = St[:, t, :].rearrange("q (a b d) -> q a b d", a=2, b=2)
            for b_ in range(2):
                sbc = S5[:, :, b_, :].unsqueeze(1).to_broadcast([128, nh, 2, 16])
                p.pool(lambda e, b_=b_, sbc=sbc: e.tensor_tensor(t25[:, :, :, b_, :], q5[:, :, :, 1 - b_, :], sbc, ALU.mult),
                       reads=[bqf, btab], writes=[bt2])
            p.dve(lambda e: e.tensor_tensor(dst, t1[:, 0:W], t2[:, 0:W], ALU.add), reads=[bt1, bt2], writes=[bdst])

        def pass_kv(l):
            with contextlib.ExitStack() as st:
                sb, ps = mk(st)
                wt, bwt = sb("wt", [128, KC, 1792], BF16)
                wn, bwn = sb("wn", [128, KC, 512], BF16)
                with contextlib.ExitStack() as st2:
                    load_w(st2, wt, bwt, 0, win_src(l, C_RK), 1024, "a")
                    load_w(st2, wt, bwt, 1024, win_src(l, C_NV), 512, "b")
                    load_w(st2, wt, bwt, 1536, win_src(l, C_GK), 256, "c")
                    load_w(st2, wn, bwn, 0, win_src(l, C_NK), 512, "d")
                    p.flush()
                Ct, bCt = sb("Ct", [128, NTI, 64], F32)
                St, _ = sb("St", [128, NTI, 64], F32)
                p.dma(Ct[:], ropeC.rearrange("(n q) d -> q n d", q=128), writes=[bCt])
                p.dma(St[:], ropeS.rearrange("(n q) d -> q n d", q=128), writes=[bCt])
                kw, bkw = sb("kw", [128, 64], F32)
                p.dma(kw[:], kn_w[l].partition_broadcast(128), writes=[bkw])
                hx = [sb("hx%d" % i, [128, KC, 512], BF16) for i in range(2)]
                sk = [sb("sk%d" % i, [128, 512], BF16) for i in range(2)]
                sv = [sb("sv%d" % i, [128, 512], BF16) for i in range(2)]
                snv = [sb("snv%d" % i, [128, 8, 128], BF16) for i in range(2)]
                sgv = [sb("sgv%d" % i, [128, 2, 128], BF16) for i in range(2)]
                for i in range(2):
                    p.pool(lambda e, i=i: e.memset(snv[i][0][:], 1.0), writes=[snv[i][1]])
                    p.pool(lambda e, i=i: e.memset(sgv[i][0][:], 1.0), writes=[sgv[i][1]])
                tmp = [sb("rr_qf", [128, 128], F32), sb("rr_sq", [128, 128], F32), sb("rr_ss", [128, 2], F32),
                       sb("rr_t1", [128, 128], F32), sb("rr_t2", [128, 128], F32)]
                kr, bkr = sb("kr", [128, 128], BF16)
                kT, bkT = sb("kT", [128, 128], BF16)
                nkT = [sb("nkT%d" % i, [128, 4, 512], BF16) for i in range(2)]
                pa = [ps("pa%d" % i, [128, 512]) for i in range(4)]
                pn = [ps("pn%d" % i, [128, 512]) for i in range(2)]
                pk, bpk = ps("pk", [128, 128], BF16)
                NS = (TT + 511) // 512

                def ld(s):
                    t0 = s * 512
                    nt = min(512, TT - t0)
                    h_, bh = hx[s % 2]
                    p.dma(h_[:, :, 0:nt], hxT[:, t0:t0 + nt].rearrange("(k q) t -> q k t", q=128), reads=[bxs], writes=[bh])
                ld(0)
                cnt = 0
                for s in range(NS):
                    t0 = s * 512
                    nt = min(512, TT - t0)
                    h_, bh = hx[s % 2]
                    if s + 1 < NS:
                        ld(s + 1)
                    nk_, bnk = nkT[s % 2]
                    for c in range(4):
                        pp, bpp = pn[c % 2]
                        for k in range(KC):
                            p.pe(lambda e, pp=pp, c=c, k=k, h_=h_, nt=nt: e.matmul(pp[:, 0:nt], lhsT=wn[:, k, c * 128:(c + 1) * 128],
                                                                                 rhs=h_[:, k, 0:nt], start=(k == 0), stop=(k == KC - 1)),
                                 reads=[bwn, bh], writes=[bpp])
                        p.act(lambda e, nk_=nk_, pp=pp, c=c, nt=nt: e.copy(nk_[:, c, 0:nt], pp[:, 0:nt]), reads=[bpp], writes=[bnk])
                    p.dma(naKT[:, t0:t0 + nt].rearrange("(c q) t -> q c t", q=128), nk_[:, :, 0:nt], reads=[bnk], writes=[bxs])
                    for j in range(nt // 128):
                        t = (t0 // 128) + j
                        tsl = slice(j * 128, (j + 1) * 128)
                        groups = [(0, 512), (512, 512), (1024, 512), (1536, 256)]
                        pps = []
                        for gi, (c0, cw) in enumerate(groups):
                            pp, bpp = pa[gi]
                            for k in range(KC):
                                p.pe(lambda e, pp=pp, c0=c0, cw=cw, k=k, h_=h_, tsl=tsl: e.matmul(pp[:, 0:cw], lhsT=h_[:, k, tsl],
                                                                                               rhs=wt[:, k, c0:c0 + cw], start=(k == 0), stop=(k == KC - 1)),
                                     reads=[bwt, bh], writes=[bpp])
                            pps.append((pp, bpp))
                        i2 = cnt % 2
                        cnt += 1
                        sk_, bsk = sk[i2]
                        sv_, bsv = sv[i2]
                        snv_, bsnv = snv[i2]
                        sgv_, bsgv = sgv[i2]
                        p.act(lambda e, sk_=sk_, pp=pps[0][0]: e.activation(sk_[:], pp[:], AF.Copy, scale=128.0 ** -0.5),
                              reads=[pps[0][1]], writes=[bsk])
                        p.dve(lambda e, sv_=sv_, pp=pps[1][0]: e.tensor_copy(sv_[:], pp[:]), reads=[pps[1][1]], writes=[bsv])
                        p.dve(lambda e, snv_=snv_, pp=pps[2][0]: e.tensor_copy(snv_[:, :, 0:64], pp[:].rearrange("q (h d) -> q h d", d=64)),
                              reads=[pps[2][1]], writes=[bsnv])
                        p.act(lambda e, sgv_=sgv_, pp=pps[3][0]: e.copy(sgv_[:, :, 0:64], pp[:, 128:256].rearrange("q (h d) -> q h d", d=64)),
                              reads=[pps[3][1]], writes=[bsgv])
                        rows = slice(t * 128, (t + 1) * 128)
                        p.dma(retK[rows, :], sk_[:], reads=[bsk], writes=[bxs])
                        p.dma(retV[rows, :], sv_[:], reads=[bsv], writes=[bxs])
                        p.dma(naV[rows, :], snv_[:].rearrange("q h d -> q (h d)"), reads=[bsnv], writes=[bxs])
                        p.dma(gV[rows, :], sgv_[:].rearrange("q h d -> q (h d)"), reads=[bsgv], writes=[bxs])
                        rms_rope(tmp, pps[3][0][:, 0:128], pps[3][1], 2, kw, bkw, Ct, St, bCt, t, kr[:], bkr, "k")
                        p.pe(lambda e: e.transpose(pk[:], kr[:], ident[:]), reads=[bkr, b_ident], writes=[bpk])
                        p.act(lambda e: e.copy(kT[:], pk[:]), reads=[bpk], writes=[bkT])
                        p.dma(gKT[:, rows], kT[:], reads=[bkT], writes=[bxs])
                p.flush()

        def ret_consts(l, sb):
            lg, blg = sb("lg", [128, 8], F32)
            p.dma(lg[:], rld[l].partition_broadcast(128), writes=[blg])
            p.act(lambda e: e.activation(lg[:], lg[:], AF.Exp), reads=[blg], writes=[blg])
            p.dve(lambda e: e.tensor_scalar(lg[:], lg[:], -1.0, None, ALU.mult), reads=[blg], writes=[blg])
            rp, brp = sb("rp", [128, 8], F32)
            p.dma(rp[:], rpos, writes=[brp])
            tb, btb = sb("tb", [128, 3, 8], F32)
            for d in range(2):
                for h in range(4):
                    i = d * 4 + h
                    for kind, col in ((0, d), (1, 2 + d), (2, 4)):
                        p.act(lambda e, kind=kind, i=i, col=col: e.activation(tb[:, kind, i:i + 1], rp[:, col:col + 1], AF.Exp, scale=lg[:, i:i + 1]),
                              reads=[blg, brp], writes=[btb])
            return tb, btb, lg, blg

        def pass_retb(l):
            with contextlib.ExitStack() as st:
                sb, ps = mk(st)
                tb, btb, lg, blg = ret_consts(l, sb)
                S, bS = sb("Sb", [128, 4, 128], F32)
                p.pool(lambda e: e.memset(S[:], 0.0), writes=[bS])
                Kt = [sb("Kt%d" % i, [128, 512], BF16) for i in range(2)]
                Vt = [sb("Vt%d" % i, [128, 512], BF16) for i in range(2)]
                Vz = [sb("Vz%d" % i, [128, 4, 128], BF16) for i in range(2)]
                Sst = [sb("Sst%d" % i, [128, 512], BF16) for i in range(2)]
                pu = [ps("pu%d" % i, [128, 4, 128]) for i in range(2)]
                order = list(range(NTI - 1, -1, -1))

                def ld(ii):
                    n = order[ii]
                    p.dma(Kt[ii % 2][0][:], retK[n * 128:(n + 1) * 128, :], reads=[bxs], writes=[Kt[ii % 2][1]])
                    p.dma(Vt[ii % 2][0][:], retV[n * 128:(n + 1) * 128, :], reads=[bxs], writes=[Vt[ii % 2][1]])
                ld(0)
                for ii, n in enumerate(order):
                    if ii + 1 < len(order):
                        ld(ii + 1)
                    K_, bK = Kt[ii % 2]
                    V_, bV = Vt[ii % 2]
                    Vz_, bVz = Vz[ii % 2]
                    ss_, bss = Sst[ii % 2]
                    pu_, bpu = pu[ii % 2]
                    p.act(lambda e, ss_=ss_: e.copy(ss_[:], S[:].rearrange("q h d -> q (h d)")), reads=[bS], writes=[bss])
                    p.dma(SbP[n], ss_[:], reads=[bss], writes=[bxs])
                    for h in range(4):
                        p.dve(lambda e, h=h, Vz_=Vz_, V_=V_: e.tensor_scalar(Vz_[:, h, :], V_[:, h * 128:(h + 1) * 128], tb[:, 0, 4 + h:5 + h], None, ALU.mult),
                              reads=[bV, btb], writes=[bVz])
                    for h in range(4):
                        p.pe(lambda e, h=h, pu_=pu_, K_=K_, Vz_=Vz_: e.matmul(pu_[:, h, :], lhsT=K_[:, h * 128:(h + 1) * 128], rhs=Vz_[:, h, :], start=True, stop=True),
                             reads=[bK, bVz], writes=[bpu])
                    for h in range(4):
                        p.dve(lambda e, h=h, pu_=pu_: e.scalar_tensor_tensor(S[:, h, :], S[:, h, :], tb[:, 2, 4 + h:5 + h], pu_[:, h, :], ALU.mult, ALU.add),
                              reads=[bS, bpu, btb], writes=[bS])
                p.flush()

        def pass_retf(l):
            with contextlib.ExitStack() as st:
                sb, ps = mk(st)
                wd, bwd = sb("wd", [128, KC, 1024], BF16)
                with contextlib.ExitStack() as st2:
                    load_w(st2, wd, bwd, 0, win_src(l, C_RQ), 1024, "a")
                    p.flush()
                tb, btb, lg, blg = ret_consts(l, sb)
                DT, bDT = sb("DT", [128, 4, 128], F32)
                cm = [sb("cm%d" % i, [128, 128], F32) for i in range(4)]
                for i, src in enumerate((dmf_in, dmb_in, mkf_in, mkb_in)):
                    p.dma(cm[i][0][:], src, writes=[cm[i][1]])
                e1, be1 = sb("e1", [128, 128], F32)
                e2, be2 = sb("e2", [128, 128], F32)
                for h in range(4):
                    p.act(lambda e, h=h: e.activation(e1[:], cm[0][0][:], AF.Exp, scale=lg[:, h:h + 1]), reads=[cm[0][1], blg], writes=[be1])
                    p.act(lambda e, h=h: e.activation(e2[:], cm[1][0][:], AF.Exp, scale=lg[:, 4 + h:5 + h]), reads=[cm[1][1], blg], writes=[be2])
                    p.dve(lambda e: e.tensor_tensor(e1[:], e1[:], cm[2][0][:], ALU.mult), reads=[be1, cm[2][1]], writes=[be1])
                    p.dve(lambda e: e.tensor_tensor(e2[:], e2[:], cm[3][0][:], ALU.mult), reads=[be2, cm[3][1]], writes=[be2])
                    p.dve(lambda e, h=h: e.tensor_tensor(DT[:, h, :], e1[:], e2[:], ALU.add), reads=[be1, be2], writes=[bDT])
                Xf, bXf = sb("Xf", [128, 4, 128], F32)
                Xb, bXb = sb("Xb", [128, 4, 128], F32)
                for h in range(4):
                    p.dve(lambda e, h=h: e.tensor_scalar(Xf[:, h, :], onesb[:], tb[:, 1, h:h + 1], None, ALU.mult), reads=[btb, b_ones], writes=[bXf])
                    p.dve(lambda e, h=h: e.tensor_scalar(Xb[:, h, :], onesb[:], tb[:, 1, 4 + h:5 + h], None, ALU.mult), reads=[btb, b_ones], writes=[bXb])
                gw, bgw = sb("gw", [128, 512], F32)
                p.dma(gw[:], gn_w[l].partition_broadcast(128), writes=[bgw])
                S, bS = sb("Sf", [128, 4, 128], F32)
                Sbf, bSbf = sb("Sfb", [128, 4, 128], BF16)
                p.pool(lambda e: e.memset(S[:], 0.0), writes=[bS])
                p.pool(lambda e: e.memset(Sbf[:], 0.0), writes=[bSbf])
                hx = [sb("hx%d" % i, [128, KC, 128], BF16) for i in range(2)]
                Kt = [sb("Kt%d" % i, [128, 512], BF16) for i in range(2)]
                Vt = [sb("Vt%d" % i, [128, 512], BF16) for i in range(2)]
                Sp = [sb("Sp%d" % i, [128, 512], BF16) for i in range(2)]
                QT, bQT = sb("QT", [128, 4, 128], BF16)
                KT, bKT = sb("KT", [128, 4, 128], BF16)
                SD, bSD = sb("SD", [128, 4, 128], BF16)
                o, bo = sb("o", [128, 4, 128], F32)
                tt, btt = sb("tt", [128, 4, 128], F32)
                sq, bsq = sb("sq", [128, 4, 128], F32)
                st4, bst4 = sb("st4", [128, 4, 4], F32)
                yb, byb = sb("yb", [128, 4, 128], BF16)
                yT, byT = sb("yT", [128, 4, 128], BF16)
                Vz, bVz = sb("Vz", [128, 4, 128], BF16)
                pq, bpq = ps("pq", [128, 4, 128])
                pkk, bpkk = ps("pkk", [128, 4, 128])
                psc, bpsc = ps("psc", [128, 4, 128])
                pin, bpin = ps("pin", [128, 4, 128])
                pf, bpf = ps("pf", [128, 4, 128])
                pb, bpb = ps("pb", [128, 4, 128])
                pu, bpu = ps("pu", [128, 4, 128])
                pT, bpT = ps("pT", [128, 4, 128], BF16)
                order = [NTL, NTL + 1] + list(range(NTL))

                def ld(ii):
                    n = order[ii]
                    rows = slice(n * 128, (n + 1) * 128)
                    p.dma(hx[ii % 2][0][:], hxT[:, rows].rearrange("(k q) t -> q k t", q=128), reads=[bxs], writes=[hx[ii % 2][1]])
                    p.dma(Kt[ii % 2][0][:], retK[rows, :], reads=[bxs], writes=[Kt[ii % 2][1]])
                    p.dma(Vt[ii % 2][0][:], retV[rows, :], reads=[bxs], writes=[Vt[ii % 2][1]])
                    p.dma(Sp[ii % 2][0][:], SbP[n], reads=[bxs], writes=[Sp[ii % 2][1]])
                ld(0)
                for ii, n in enumerate(order):
                    if ii + 1 < len(order):
                        ld(ii + 1)
                    h_, bh = hx[ii % 2]
                    K_, bK = Kt[ii % 2]
                    V_, bV = Vt[ii % 2]
                    Sp_, bSp = Sp[ii % 2]
                    for c in range(8):
                        dst, bd = (pq, bpq) if c < 4 else (pkk, bpkk)
                        for k in range(KC):
                            p.pe(lambda e, dst=dst, c=c, k=k, h_=h_: e.matmul(dst[:, c % 4, :], lhsT=wd[:, k, c * 128:(c + 1) * 128], rhs=h_[:, k, :],
                                                                            start=(k == 0), stop=(k == KC - 1)), reads=[bwd, bh], writes=[bd])
                    p.act(lambda e: e.copy(QT[:], pq[:]), reads=[bpq], writes=[bQT])
                    p.act(lambda e: e.activation(KT[:], pkk[:], AF.Copy, scale=128.0 ** -0.5), reads=[bpkk], writes=[bKT])
                    for h in range(4):
                        p.pe(lambda e, h=h: e.matmul(psc[:, h, :], lhsT=KT[:, h, :], rhs=QT[:, h, :], start=True, stop=True),
                             reads=[bKT, bQT], writes=[bpsc])
                    p.dve(lambda e: e.tensor_tensor(SD[:], psc[:], DT[:], ALU.mult), reads=[bpsc, bDT], writes=[bSD])
                    for h in range(4):
                        p.pe(lambda e, h=h, V_=V_: e.matmul(pin[:, h, :], lhsT=SD[:, h, :], rhs=V_[:, h * 128:(h + 1) * 128], start=True, stop=True),
                             reads=[bSD, bV], writes=[bpin])
                        p.pe(lambda e, h=h: e.matmul(pf[:, h, :], lhsT=QT[:, h, :], rhs=Sbf[:, h, :], start=True, stop=True),
                             reads=[bQT, bSbf], writes=[bpf])
                        p.pe(lambda e, h=h, Sp_=Sp_: e.matmul(pb[:, h, :], lhsT=QT[:, h, :], rhs=Sp_[:, h * 128:(h + 1) * 128], start=True, stop=True),
                             reads=[bQT, bSp], writes=[bpb])
                    p.act(lambda e: e.copy(o[:], pin[:]), reads=[bpin], writes=[bo])
                    p.dve(lambda e: e.tensor_tensor(tt[:], pf[:], Xf[:], ALU.mult), reads=[bpf, bXf], writes=[btt])
                    p.pool(lambda e: e.tensor_tensor(o[:], o[:], tt[:], ALU.add), reads=[bo, btt], writes=[bo])
                    p.dve(lambda e: e.tensor_tensor(tt[:], pb[:], Xb[:], ALU.mult), reads=[bpb, bXb], writes=[btt])
                    p.pool(lambda e: e.tensor_tensor(o[:], o[:], tt[:], ALU.add), reads=[bo, btt], writes=[bo])
                    p.dve(lambda e: e.tensor_reduce(st4[:, 0, :], o[:], AX.X, ALU.add), reads=[bo], writes=[bst4])
                    p.pool(lambda e: e.tensor_tensor(sq[:], o[:], o[:], ALU.mult), reads=[bo], writes=[bsq])
                    p.dve(lambda e: e.tensor_reduce(st4[:, 1, :], sq[:], AX.X, ALU.add), reads=[bsq, bst4], writes=[bst4])
                    p.dve(lambda e: e.tensor_scalar(st4[:, 2, :], st4[:, 0, :], 1.0 / 128, None, ALU.mult), reads=[bst4], writes=[bst4])
                    p.dve(lambda e: e.tensor_tensor(st4[:, 0, :], st4[:, 2, :], st4[:, 2, :], ALU.mult), reads=[bst4], writes=[bst4])
                    p.dve(lambda e: e.scalar_tensor_tensor(st4[:, 3, :], st4[:, 1, :], 1.0 / 128, st4[:, 0, :], ALU.mult, ALU.subtract),
                          reads=[bst4], writes=[bst4])
                    p.dve(lambda e: e.tensor_scalar(st4[:, 3, :], st4[:, 3, :], EPS, None, ALU.add), reads=[bst4], writes=[bst4])
                    p.act(lambda e: e.activation(st4[:, 3, :], st4[:, 3, :], AF.Sqrt), reads=[bst4], writes=[bst4])
                    p.dve(lambda e: e.reciprocal(st4[:, 3, :], st4[:, 3, :]), reads=[bst4], writes=[bst4])
                    p.dve(lambda e: e.tensor_tensor(o[:], o[:], st4[:, 2, :].unsqueeze(2).to_broadcast([128, 4, 128]), ALU.subtract),
                          reads=[bo, bst4], writes=[bo])
                    p.dve(lambda e: e.tensor_tensor(o[:], o[:], st4[:, 3, :].unsqueeze(2).to_broadcast([128, 4, 128]), ALU.mult),
                          reads=[bo, bst4], writes=[bo])
                    p.pool(lambda e: e.tensor_tensor(yb[:], o[:], gw[:].rearrange("q (h d) -> q h d", d=128), ALU.mult), reads=[bo, bgw], writes=[byb])
                    for h in range(4):
                        p.pe(lambda e, h=h: e.transpose(pT[:, h, :], yb[:, h, :], ident[:]), reads=[byb, b_ident], writes=[bpT])
                    p.act(lambda e: e.copy(yT[:], pT[:]), reads=[bpT], writes=[byT])
                    p.dma(retXT[:, n * 128:(n + 1) * 128].rearrange("(h q) t -> q h t", q=128), yT[:], reads=[byT], writes=[bxs])
                    for h in range(4):
                        p.dve(lambda e, h=h, V_=V_: e.tensor_scalar(Vz[:, h, :], V_[:, h * 128:(h + 1) * 128], tb[:, 0, h:h + 1], None, ALU.mult),
                              reads=[bV, btb], writes=[bVz])
                    for h in range(4):
                        p.pe(lambda e, h=h, K_=K_: e.matmul(pu[:, h, :], lhsT=K_[:, h * 128:(h + 1) * 128], rhs=Vz[:, h, :], start=True, stop=True),
                             reads=[bK, bVz], writes=[bpu])
                    for h in range(4):
                        p.dve(lambda e, h=h: e.scalar_tensor_tensor(S[:, h, :], S[:, h, :], tb[:, 2, h:h + 1], pu[:, h, :], ALU.mult, ALU.add),
                              reads=[bS, bpu, btb], writes=[bS])
                    p.act(lambda e: e.copy(Sbf[:], S[:]), reads=[bS], writes=[bSbf])
                p.flush()

        class Attn:
            def __init__(self, sb, ps):
                self.S = [ps("aS%d" % i, [128, 512]) for i in range(4)]
                self.acc = [ps("aA%d" % i, [128, 512]) for i in range(2)]
                self.P = [sb("aP%d" % i, [128, 512], BF16) for i in range(4)]
                self.T = [sb("aT%d" % i, [128, 512], F32) for i in range(2)]
                self.rec, self.brec = sb("arec", [64, 512], F32)
                self.rec2, self.brec2 = sb("arec2", [128, 512], F32)
                self.ost = [sb("aO%d" % i, [64, 512], BF16) for i in range(2)]
                self.n = 0
                self.nt = 0

            def run(self, qT, bq, nq, keys, scale):
                acc, bacc = self.acc[self.n % 2]
                ost, bost = self.ost[self.n % 2]
                self.n += 1
                nk = len(keys)
                slots = [None] * nk

                def qk(i):
                    kT, bk, vx, bv, bias = keys[i]
                    s_, bs = self.S[self.nt % 4]
                    pt_, bp = self.P[self.nt % 4]
                    t_, bt = self.T[self.nt % 2]
                    self.nt += 1
                    slots[i] = (pt_, bp)
                    p.pe(lambda e: e.matmul(s_[:, 0:nq], lhsT=kT, rhs=qT, start=True, stop=True), reads=[bk, bq], writes=[bs])
                    if bias is None:
                        p.act(lambda e: e.activation(pt_[:, 0:nq], s_[:, 0:nq], AF.Exp, scale=scale), reads=[bs], writes=[bp])
                    else:
                        bap, bb_, map_, mb_ = bias
                        p.dve(lambda e: e.scalar_tensor_tensor(t_[:, 0:nq], s_[:, 0:nq], scale, bap, ALU.mult, ALU.add),
                              reads=[bs, bb_], writes=[bt])
                        if map_ is not None:
                            p.pool(lambda e: e.tensor_tensor(t_[:, 0:nq], t_[:, 0:nq], map_, ALU.add), reads=[bt, mb_], writes=[bt])
                        p.act(lambda e: e.activation(pt_[:, 0:nq], t_[:, 0:nq], AF.Exp), reads=[bt], writes=[bp])

                def pv(i):
                    kT, bk, vx, bv, bias = keys[i]
                    pt_, bp = slots[i]
                    p.pe(lambda e: e.matmul(acc[:, 0:nq], lhsT=vx, rhs=pt_[:, 0:nq], start=(i == 0), stop=(i == nk - 1)),
                         reads=[bv, bp], writes=[bacc])
                LA = 2
                for i in range(min(LA, nk)):
                    qk(i)
                for i in range(nk):
                    if i + LA < nk:
                        qk(i + LA)
                    pv(i)
                rec, brec = self.rec, self.brec
                rec2, brec2 = self.rec2, self.brec2
                p.dve(lambda e: e.reciprocal(rec2[64:128, 0:nq], acc[64:128, 0:nq]), reads=[bacc], writes=[brec2])
                p.pool(lambda e: e.tensor_copy(rec[:, 0:nq], rec2[64:128, 0:nq]), reads=[brec2], writes=[brec])
                p.dve(lambda e: e.tensor_tensor(ost[:, 0:nq], acc[0:64, 0:nq], rec[:, 0:nq], ALU.mult), reads=[bacc, brec], writes=[bost])
                return ost, bost

        def pass_na(l):
            with contextlib.ExitStack() as st:
                sb, ps = mk(st)
                wq, bwq = sb("wq", [128, KC, 512], BF16)
                BI, bBI = sb("BI", [128, 8, 22 * 64], BF16)
                BN, bBN = sb("BN", [128, 8, 22 * 64], BF16)
                ME, bME = sb("ME", [128, 12, 512], F32)
                with contextlib.ExitStack() as st2:
                    sb2, _ = mk(st2)
                    load_w(st2, wq, bwq, 0, win_src(l, C_NQ), 512, "a")
                    mi, bmi = sb2("mi", [128, 22 * 64], F32)
                    p.dma(mi[:], maskI_in, writes=[bmi])
                    p.dma(ME[:].rearrange("q e n -> q (e n)"), maskE_in, writes=[bME])
                    rst = [sb2("rst%d" % i, [128, 22 * 64], F32) for i in range(2)]
                    for h in range(8):
                        r_, br = rst[h % 2]
                        p.dma(r_[:], rpbg[l, :, h * 1408:(h + 1) * 1408], writes=[br])
                        p.dve(lambda e, h=h, r_=r_: e.tensor_tensor(BI[:, h, :], r_[:], mi[:], ALU.add), reads=[br, bmi], writes=[bBI])
                        p.pool(lambda e, h=h, r_=r_: e.tensor_copy(BN[:, h, :], r_[:]), reads=[br], writes=[bBN])
                    p.flush()
                at = Attn(sb, ps)
                hx = [sb("hx%d" % i, [128, KC, 512], BF16) for i in range(2)]
                KTw = [sb("KTw%d" % i, [128, 4, 1024], BF16) for i in range(2)]
                Vw = [sb("Vw%d" % i, [128, 8, 1024], BF16) for i in range(2)]
                KTc, bKTc = sb("KTc", [128, 4, 256], BF16)
                Vc, bVc = sb("Vc", [128, 2, 1024], BF16)
                QT = [sb("QT%d" % i, [128, 4, 512], BF16) for i in range(2)]
                pq = [ps("pq%d" % i, [128, 512]) for i in range(2)]
                p.dma(KTc[:], naKT[:, NT:TT].rearrange("(c q) t -> q c t", q=128), reads=[bxs], writes=[bKTc])
                p.dma(Vc[:], naV[NT:TT, :].rearrange("(n q) d -> q n d", q=128), reads=[bxs], writes=[bVc])
                blocks = list(range(NRB)) + ["ctx"]

                def urange(i):
                    return [u for u in range(8) if 0 <= 4 * i - 2 + u < NTL]

                def ld(bi):
                    i = blocks[bi]
                    h_, bh = hx[bi % 2]
                    if i == "ctx":
                        p.dma(h_[:, :, 0:L], hxT[:, NT:TT].rearrange("(k q) t -> q k t", q=128), reads=[bxs], writes=[bh])
                        return
                    p.dma(h_[:], hxT[:, i * 512:(i + 1) * 512].rearrange("(k q) t -> q k t", q=128), reads=[bxs], writes=[bh])
                    us = urange(i)
                    u0, u1 = us[0], us[-1] + 1
                    tk0 = 4 * i - 2 + u0
                    ntk = u1 - u0
                    kw_, bkw_ = KTw[bi % 2]
                    vw_, bvw_ = Vw[bi % 2]
                    p.dma(kw_[:, :, u0 * 128:u1 * 128], naKT[:, tk0 * 128:(tk0 + ntk) * 128].rearrange("(c q) t -> q c t", q=128),
                          reads=[bxs], writes=[bkw_])
                    p.dma(vw_[:, u0:u1, :], naV[tk0 * 128:(tk0 + ntk) * 128, :].rearrange("(n q) d -> q n d", q=128),
                          reads=[bxs], writes=[bvw_])
                ld(0)
                for bi, i in enumerate(blocks):
                    if bi + 1 < len(blocks):
                        ld(bi + 1)
                    h_, bh = hx[bi % 2]
                    kw_, bkw_ = KTw[bi % 2]
                    vw_, bvw_ = Vw[bi % 2]
                    q_, bq_ = QT[bi % 2]
                    nq = L if i == "ctx" else 512
                    for c in range(4):
                        pp, bpp = pq[c % 2]
                        for k in range(KC):
                            p.pe(lambda e, pp=pp, c=c, k=k, h_=h_, nq=nq: e.matmul(pp[:, 0:nq], lhsT=wq[:, k, c * 128:(c + 1) * 128], rhs=h_[:, k, 0:nq],
                                                                                 start=(k == 0), stop=(k == KC - 1)), reads=[bwq, bh], writes=[bpp])
                        p.act(lambda e, pp=pp, c=c, q_=q_, nq=nq: e.copy(q_[:, c, 0:nq], pp[:, 0:nq]), reads=[bpp], writes=[bq_])
                    for h in range(8):
                        ps_ = slice((h % 2) * 64, (h % 2) * 64 + 64)
                        ch = h // 2
                        keys = []
                        if i != "ctx":
                            edge = (i == 0) or (i == NRB - 1)
                            for u in urange(i):
                                m0 = 14 - 2 * u
                                if not edge:
                                    bias = (BI[:, h, m0 * 64:(m0 + 8) * 64], bBI, None, None)
                                else:
                                    e_ = (u - 2) if i == 0 else (6 + u)
                                    bias = (BN[:, h, m0 * 64:(m0 + 8) * 64], bBN, ME[:, e_, :], bME)
                                keys.append((kw_[ps_, ch, u * 128:(u + 1) * 128], bkw_, vw_[:, u, h * 128:(h + 1) * 128], bvw_, bias))
                        for u in range(2):
                            keys.append((KTc[ps_, ch, u * 128:(u + 1) * 128], bKTc, Vc[:, u, h * 128:(h + 1) * 128], bVc, None))
                        ost, bost = at.run(q_[ps_, ch, 0:nq], bq_, nq, keys, 0.125)
                        t0 = NT if i == "ctx" else i * 512
                        p.dma(naXT[h * 64:(h + 1) * 64, t0:t0 + nq], ost[:, 0:nq], reads=[bost], writes=[bxs])
                p.flush()

        def pass_gqa(l):
            with contextlib.ExitStack() as st:
                sb, ps = mk(st)
                wq, bwq = sb("wq", [128, KC, 512], BF16)
                with contextlib.ExitStack() as st2:
                    def gq_perm(dst, src):
                        d4 = dst.rearrange("r (p g d) -> r p g d", p=4, g=2, d=64)
                        s4 = src.rearrange("r (g p d) -> r p g d", p=4, g=2, d=64)
                        return [(d4[:, :, g, :], s4[:, :, g, :]) for g in range(2)]
                    load_w(st2, wq, bwq, 0, win_src(l, C_GQ), 512, "a", perm=gq_perm)
                    p.flush()
                Ct, bCt = sb("Ct", [128, NTI, 64], F32)
                St, _ = sb("St", [128, NTI, 64], F32)
                p.dma(Ct[:], ropeC.rearrange("(n q) d -> q n d", q=128), writes=[bCt])
                p.dma(St[:], ropeS.rearrange("(n q) d -> q n d", q=128), writes=[bCt])
                qw, bqw = sb("qw", [128, 64], F32)
                p.dma(qw[:], qn_w[l].partition_broadcast(128), writes=[bqw])
                KTa, bKTa = sb("KTa", [128, TT], BF16)
                Va, bVa = sb("Va", [128, NTI, 256], BF16)
                p.dma(KTa[:], gKT, reads=[bxs], writes=[bKTa])
                p.dma(Va[:], gV.rearrange("(n q) d -> q n d", q=128), reads=[bxs], writes=[bVa])
                at = Attn(sb, ps)
                hx = [sb("hx%d" % i, [128, KC, 128], BF16) for i in range(2)]
                tmp = [sb("rr_qf", [128, 512], F32), sb("rr_sq", [128, 512], F32), sb("rr_ss", [128, 8], F32),
                       sb("rr_t1", [128, 512], F32), sb("rr_t2", [128, 512], F32)]
                qr, bqr = sb("qr", [128, 512], BF16)
                QTa = [sb("QTa%d" % i, [128, 4, 128], BF16) for i in range(2)]
                pq, bpq = ps("pq", [128, 512])
                pT, bpT = ps("pT", [128, 4, 128], BF16)

                def ld(t):
                    p.dma(hx[t % 2][0][:], hxT[:, t * 128:(t + 1) * 128].rearrange("(k q) t -> q k t", q=128), reads=[bxs], writes=[hx[t % 2][1]])
                ld(0)
                for t in range(NTI):
                    if t + 1 < NTI:
                        ld(t + 1)
                    h_, bh = hx[t % 2]
                    for k in range(KC):
                        p.pe(lambda e, k=k, h_=h_: e.matmul(pq[:], lhsT=h_[:, k, :], rhs=wq[:, k, :], start=(k == 0), stop=(k == KC - 1)),
                             reads=[bwq, bh], writes=[bpq])
                    rms_rope(tmp, pq[:], bpq, 8, qw, bqw, Ct, St, bCt, t, qr[:], bqr, "q")
                    for pp in range(4):
                        p.pe(lambda e, pp=pp: e.transpose(pT[:, pp, :], qr[:, pp * 128:(pp + 1) * 128], ident[:]), reads=[bqr, b_ident], writes=[bpT])
                    qa, bqa = QTa[t % 2]
                    p.act(lambda e, qa=qa: e.copy(qa[:], pT[:]), reads=[bpT], writes=[bqa])
                    ktiles = list(range(NTI)) if t < NTL else [NTL, NTL + 1]
                    for g in range(2):
                        ps_ = slice(g * 64, (g + 1) * 64)
                        keys = [(KTa[ps_, tk * 128:(tk + 1) * 128], bKTa, Va[:, tk, g * 128:(g + 1) * 128], bVa, None) for tk in ktiles]
                        ost, bost = at.run(qa[ps_, :, :].rearrange("q a t -> q (a t)"), bqa, 512, keys, 0.125)
                        for pp in range(4):
                            hh = 4 * g + pp
                            p.dma(gqaXT[hh * 64:(hh + 1) * 64, t * 128:(t + 1) * 128], ost[:, pp * 128:(pp + 1) * 128], reads=[bost], writes=[bxs])
                p.flush()

        def pass_merge(l, last):
            with contextlib.ExitStack() as st:
                sb, ps = mk(st)
                wg, bwg = sb("wg", [128, KC, 4608], BF16)
                wo, bwo = sb("wo", [128, 12, D], BF16)
                wu, bwu = sb("wu", [128, KC, D], BF16)
                with contextlib.ExitStack() as st2:
                    load_w(st2, wg, bwg, 0, win_src(l, C_RG), 512, "a")
                    load_w(st2, wg, bwg, 512, win_src(l, C_NG), 512, "b")
                    load_w(st2, wg, bwg, 1024, win_src(l, C_GG), 512, "c")
                    load_w(st2, wg, bwg, 1536, win_src(l, C_MG), 3072, "d")
                    for X, w_ in enumerate((w_ro, w_no, w_go)):
                        load_w(st2, wo[:, X * 4:(X + 1) * 4, :], bwo, 0,
                               (lambda w_: (lambda k, c0, cw: w_[l, k * 128:(k + 1) * 128, c0:c0 + cw]))(w_), D, "e%d" % X, nk=4)
                    load_w(st2, wu, bwu, 0, lambda k, c0, cw: w_out[l, k * 128:(k + 1) * 128, c0:c0 + cw], D, "f")
                    p.flush()
                NQ = 256
                hx = [sb("hx%d" % i, [128, KC, NQ], BF16) for i in range(1)] * 2
                oX = [sb("oX%d" % i, [128, 12, NQ], BF16) for i in range(1)] * 2
                sg, bsg = sb("sg", [128, 12, NQ], BF16)
                sm, bsm = sb("sm", [128, 24, NQ], BF16)
                gx, bgx = sb("gx", [128, 12, NQ], BF16)
                mT, bmT = sb("mT", [128, KC, NQ], BF16)
                t0_, bt0 = sb("t0", [128, NQ], F32)
                t1_, bt1 = sb("t1", [128, NQ], F32)
                t2_, bt2 = sb("t2", [128, NQ], F32)
                xt = [sb("xt%d" % i, [128, D], F32) for i in range(2)]
                dl, bdl = sb("dl", [128, D], F32)
                pg = [ps("pg%d" % i, [128, NQ]) for i in range(3)]
                pu = [ps("pu%d" % i, [128, NQ]) for i in range(3)]
                po = [ps("po%d" % i, [128, 512]) for i in range(2)]
                NS = TT // NQ

                def ld(s):
                    c0 = s * NQ
                    p.dma(hx[s % 2][0][:], hxT[:, c0:c0 + NQ].rearrange("(k q) t -> q k t", q=128), reads=[bxs], writes=[hx[s % 2][1]])
                    for X, src in enumerate((retXT, naXT, gqaXT)):
                        p.dma(oX[s % 2][0][:, X * 4:(X + 1) * 4, :], src[:, c0:c0 + NQ].rearrange("(k q) t -> q k t", q=128),
                              reads=[bxs], writes=[oX[s % 2][1]])
                xcnt = 0
                for s in range(NS):
                    ld(s)
                    h_, bh = hx[s % 2]
                    o_, bo_ = oX[s % 2]
                    isctx = (s * NQ >= NT)
                    for c in range(36):
                        pp, bpp = pg[c % 3]
                        for k in range(KC):
                            p.pe(lambda e, pp=pp, c=c, k=k, h_=h_: e.matmul(pp[:], lhsT=wg[:, k, c * 128:(c + 1) * 128], rhs=h_[:, k, :],
                                                                          start=(k == 0), stop=(k == KC - 1)), reads=[bwg, bh], writes=[bpp])
                        if c < 12:
                            p.act(lambda e, pp=pp, c=c: e.activation(sg[:, c, :], pp[:], AF.Silu), reads=[bpp], writes=[bsg])
                        else:
                            p.act(lambda e, pp=pp, c=c: e.activation(sm[:, c - 12, :], pp[:], AF.Sigmoid), reads=[bpp], writes=[bsm])
                    p.dve(lambda e, o_=o_: e.tensor_tensor(gx[:], o_[:], sg[:], ALU.mult), reads=[bo_, bsg], writes=[bgx])
                    for oc in range(8):
                        for X in range(3):
                            pp, bpp = pu[X]
                            for k in range(4):
                                p.pe(lambda e, pp=pp, X=X, k=k, oc=oc: e.matmul(pp[:], lhsT=wo[:, X * 4 + k, oc * 128:(oc + 1) * 128], rhs=gx[:, X * 4 + k, :],
                                                                              start=(k == 0), stop=(k == 3)), reads=[bwo, bgx], writes=[bpp])
                        p.dve(lambda e, oc=oc: e.tensor_tensor(t0_[:], pu[0][0][:], sm[:, oc, :], ALU.mult), reads=[pu[0][1], bsm], writes=[bt0])
                        p.dve(lambda e, oc=oc: e.tensor_tensor(t1_[:], pu[1][0][:], sm[:, 8 + oc, :], ALU.mult), reads=[pu[1][1], bsm], writes=[bt1])
                        p.dve(lambda e, oc=oc: e.tensor_tensor(t2_[:], pu[2][0][:], sm[:, 16 + oc, :], ALU.mult), reads=[pu[2][1], bsm], writes=[bt2])
                        p.pool(lambda e: e.tensor_tensor(t0_[:], t0_[:], t1_[:], ALU.add), reads=[bt0, bt1], writes=[bt0])
                        p.pool(lambda e, oc=oc: e.tensor_tensor(mT[:, oc, :], t0_[:], t2_[:], ALU.add), reads=[bt0, bt2], writes=[bmT])
                    if DBG:
                        p.dma(DBG["mT"][s], mT[:].rearrange("q k t -> q (k t)"), reads=[bmT], writes=[bxs])
                    if isctx and last:
                        continue
                    for j in range(NQ // 128):
                        t = (s * NQ) // 128 + j
                        x_, bx = xt[xcnt % 2]
                        xcnt += 1
                        p.dma(x_[:], xtile_src(l, t), reads=[bxs], writes=[bx])
                        gslot = 5 if isctx else 2
                        for cg in range(2):
                            pp, bpp = po[cg]
                            for k in range(KC):
                                p.pe(lambda e, pp=pp, cg=cg, k=k, j=j: e.matmul(pp[:], lhsT=mT[:, k, j * 128:(j + 1) * 128], rhs=wu[:, k, cg * 512:(cg + 1) * 512],
                                                                             start=(k == 0), stop=(k == KC - 1)), reads=[bwu, bmT], writes=[bpp])
                            p.dve(lambda e, pp=pp, cg=cg, gslot=gslot: e.tensor_tensor(dl[:, cg * 512:(cg + 1) * 512], pp[:], modt[:, gslot, cg * 512:(cg + 1) * 512], ALU.mult),
                                  reads=[bpp, b_modt], writes=[bdl])
                        p.pool(lambda e, x_=x_: e.tensor_tensor(x_[:], x_[:], dl[:], ALU.add), reads=[bx, bdl], writes=[bx])
                        p.dma(xs[t * 128:(t + 1) * 128, :], x_[:], reads=[bx], writes=[bxs])
                p.flush()

        def pass_final():
            with contextlib.ExitStack() as st:
                sb, ps = mk(st)
                fw_, bfw = sb("fw", [128, D], F32)
                p.dma(fw_[:], fnw.partition_broadcast(128), writes=[bfw])
                xt = [sb("xt%d" % i, [128, D], F32) for i in range(2)]
                junk, bj = sb("junk", [128, D], F32)
                ssq = [sb("ssq%d" % i, [128, 1], F32) for i in range(2)]
                ot = [sb("ot%d" % i, [128, D], F32) for i in range(2)]
                for t in range(NTL):
                    x_, bx = xt[t % 2]
                    sq, bsq = ssq[t % 2]
                    o_, bo_ = ot[t % 2]
                    p.dma(x_[:], xs[t * 128:(t + 1) * 128, :], reads=[bxs], writes=[bx])
                    p.act(lambda e, x_=x_, sq=sq: e.activation(junk[:], x_[:], AF.Square, accum_out=sq[:]), reads=[bx], writes=[bj, bsq])
                    p.dve(lambda e, sq=sq: e.tensor_scalar(sq[:], sq[:], 1.0 / D, EPS, ALU.mult, ALU.add), reads=[bsq], writes=[bsq])
                    p.act(lambda e, sq=sq: e.activation(sq[:], sq[:], AF.Sqrt), reads=[bsq], writes=[bsq])
                    p.dve(lambda e, sq=sq: e.reciprocal(sq[:], sq[:]), reads=[bsq], writes=[bsq])
                    p.dve(lambda e, x_=x_, sq=sq, o_=o_: e.scalar_tensor_tensor(o_[:], x_[:], sq[:, 0:1], fw_[:], ALU.mult, ALU.mult),
                          reads=[bx, bsq, bfw], writes=[bo_])
                    p.dma(out[t * 128:(t + 1) * 128, :], o_[:], reads=[bo_])
                p.flush()

        pass_setup()
        for l in range(DEPTH):
            pass_adaln(l)
            pass_norm(l)
            pass_kv(l)
            pass_retb(l)
            pass_retf(l)
            pass_na(l)
            pass_gqa(l)
            pass_merge(l, l == DEPTH - 1)
        pass_final()
    return nc


_CACHE = {}


def kernel(x, c, ctx, c_ctx, ada_w, ada_b, norm_w, w_in, ret_log_decay, ret_gn_w, na_rpb,
           q_norm_w, k_norm_w, w_ret_o, w_na_o, w_gqa_o, w_out, final_norm_w):
    f = lambda a: np.ascontiguousarray(np.asarray(a, dtype=np.float32))
    x = f(x)
    B, NT, _ = x.shape
    DEPTH = ada_w.shape[0]
    key = (NT, DEPTH)
    if key not in _CACHE:
        _CACHE[key] = (build(NT, DEPTH), host_consts(NT))
    nc, cst = _CACHE[key]
    shared = {
        "c_ctx": f(c_ctx), "ada_w": f(ada_w), "ada_b": f(ada_b), "norm_w": f(norm_w), "w_in": f(w_in),
        "ret_log_decay": f(ret_log_decay).reshape(DEPTH, 8), "ret_gn_w": f(ret_gn_w),
        "rpbg": gather_rpb(f(na_rpb)), "q_norm_w": f(q_norm_w), "k_norm_w": f(k_norm_w),
        "w_ret_o": f(w_ret_o), "w_na_o": f(w_na_o), "w_gqa_o": f(w_gqa_o), "w_out": f(w_out),
        "final_norm_w": f(final_norm_w),
    }
    shared.update(cst)
    c = f(c)
    ctx = f(ctx)
    in_maps = []
    for b in range(B):
        m = dict(shared)
        m["x"] = x[b]
        m["c"] = c[b]
        m["ctx"] = ctx[b]
        in_maps.append(m)
    res = run_bass_kernel_spmd(nc, in_maps, core_ids=list(range(B)))
    return np.stack([np.asarray(r["out"], dtype=np.float32) for r in res.results], axis=0)
```
